# Optimizing a Trainium2 kernel written in Bass

```python
import math
import jax, jax.numpy as jnp
from jax import lax
import numpy as np

D_MODEL = 4096
BATCH = 2
SEQ = 8192
DEPTH = 2

HEAD_DIM = 128
D_MIX = D_MODEL
GROUP_WIDTH = D_MIX // 4
SC_KERNEL = 3
SC_GROUPS = GROUP_WIDTH // HEAD_DIM
DIFF_HEADS = GROUP_WIDTH // (2 * HEAD_DIM)
DN_HEADS = GROUP_WIDTH // HEAD_DIM
DN_CONV = 4
DN_CHUNK = 64
FOX_HEADS = GROUP_WIDTH // HEAD_DIM
BLOCK_Q = 128
D_FF = ((8 * D_MODEL // 3 + 255) // 256) * 256
FFN_KERNEL = 3
EPS = 1e-6
IN_WIDTHS = (
    GROUP_WIDTH, GROUP_WIDTH, GROUP_WIDTH,
    GROUP_WIDTH, GROUP_WIDTH, GROUP_WIDTH,
    GROUP_WIDTH, GROUP_WIDTH, GROUP_WIDTH, GROUP_WIDTH,
    DN_HEADS, DN_HEADS,
    GROUP_WIDTH, GROUP_WIDTH, GROUP_WIDTH,
    FOX_HEADS,
)
D_IN = sum(IN_WIDTHS)

kernel_name = "hybrid_parallel_groups_trunk"


def rms_norm(x, g):
    xf = x.astype(jnp.float32)
    y = xf * lax.rsqrt(jnp.mean(xf * xf, axis=-1, keepdims=True) + EPS)
    return (y * g.astype(jnp.float32)).astype(x.dtype)


def l2_norm(x):
    return x * lax.rsqrt(jnp.sum(x * x, axis=-1, keepdims=True) + EPS)


def causal_dwconv(x, w):
    K = w.shape[0]
    T = x.shape[1]
    xp = jnp.pad(x, ((0, 0), (K - 1, 0), (0, 0)))
    w = w.astype(x.dtype)
    return sum(xp[:, j:j + T] * w[j] for j in range(K))


def split_columns(p):
    offsets = np.cumsum(np.array(IN_WIDTHS))[:-1].tolist()
    return jnp.split(p, offsets, axis=-1)


def diff_attention(q, k, v, lam, lam_init, norm_g):
    B, T, _ = q.shape
    H, d = DIFF_HEADS, HEAD_DIM
    nb = T // BLOCK_Q
    q = q.reshape(B, T, H, 2, d)
    k = k.reshape(B, T, H, 2, d)
    v = v.reshape(B, T, H, 2 * d)
    qb = q.reshape(B, nb, BLOCK_Q, H, 2, d).transpose(1, 0, 2, 3, 4, 5)
    scale = d ** -0.5
    kpos = jnp.arange(T)

    def block(args):
        qi, i = args
        s = jnp.einsum('bqhmd,bkhmd->bhmqk', qi, k).astype(jnp.float32) * scale
        qpos = i * BLOCK_Q + jnp.arange(BLOCK_Q)
        causal = kpos[None, :] <= qpos[:, None]
        p = jax.nn.softmax(jnp.where(causal, s, -jnp.inf), axis=-1)
        wts = p[:, :, 0] - lam * p[:, :, 1]
        return jnp.einsum('bhqk,bkhe->bqhe', wts.astype(v.dtype), v)

    o = lax.map(block, (qb, jnp.arange(nb)))
    o = o.transpose(1, 0, 2, 3, 4).reshape(B, T, H, 2 * d)
    o = rms_norm(o, norm_g) * (1.0 - lam_init)
    return o.reshape(B, T, H * 2 * d)


def forgetting_attention(q, k, v, f_logit):
    B, T, _ = q.shape
    H, d = FOX_HEADS, HEAD_DIM
    nb = T // BLOCK_Q
    q = q.reshape(B, T, H, d)
    k = k.reshape(B, T, H, d)
    v = v.reshape(B, T, H, d)
    cum = jnp.cumsum(jax.nn.log_sigmoid(f_logit.astype(jnp.float32)), axis=1)
    cum_k = cum.transpose(0, 2, 1)
    qb = q.reshape(B, nb, BLOCK_Q, H, d).transpose(1, 0, 2, 3, 4)
    cb = cum.reshape(B, nb, BLOCK_Q, H).transpose(1, 0, 3, 2)
    scale = d ** -0.5
    kpos = jnp.arange(T)

    def block(args):
        qi, ci, i = args
        s = jnp.einsum('bqhd,bkhd->bhqk', qi, k).astype(jnp.float32) * scale
        s = s + ci[..., :, None] - cum_k[:, :, None, :]
        qpos = i * BLOCK_Q + jnp.arange(BLOCK_Q)
        causal = kpos[None, :] <= qpos[:, None]
        p = jax.nn.softmax(jnp.where(causal, s, -jnp.inf), axis=-1)
        return jnp.einsum('bhqk,bkhd->bqhd', p.astype(v.dtype), v)

    o = lax.map(block, (qb, cb, jnp.arange(nb)))
    return o.transpose(1, 0, 2, 3, 4).reshape(B, T, H * d)


def chunk_gated_delta_rule(q, k, v, g, beta):
    B, T, H, dk = q.shape
    dv = v.shape[-1]
    C = DN_CHUNK
    N = T // C
    f32 = jnp.float32

    def to_chunks(t):
        return t.astype(f32).transpose(0, 2, 1, 3).reshape(B, H, N, C, t.shape[-1])

    q, k, v = to_chunks(q), to_chunks(k), to_chunks(v)
    g = g.astype(f32).transpose(0, 2, 1).reshape(B, H, N, C)
    beta = beta.astype(f32).transpose(0, 2, 1).reshape(B, H, N, C)
    q = q * dk ** -0.5
    gc = jnp.cumsum(g, axis=-1)
    idx = jnp.arange(C)
    incl = idx[:, None] >= idx[None, :]
    strict = idx[:, None] > idx[None, :]
    decay = jnp.exp(jnp.where(incl, gc[..., :, None] - gc[..., None, :], -jnp.inf))
    kb = k * beta[..., None]
    a = jnp.where(strict, jnp.einsum('bhnid,bhnjd->bhnij', kb, k) * decay, 0.0)
    tmat = a + jnp.eye(C, dtype=f32)

    def solve(rhs):
        return lax.linalg.triangular_solve(tmat, rhs, left_side=True, lower=True, unit_diagonal=True)

    w = solve(kb * jnp.exp(gc)[..., None])
    u = solve(v * beta[..., None])
    aqk = jnp.where(incl, jnp.einsum('bhnid,bhnjd->bhnij', q, k) * decay, 0.0)
    xs = tuple(jnp.moveaxis(t, 2, 0) for t in (q, k, u, w, gc, aqk))

    def step(S, inp):
        qi, ki, ui, wi, gi, ai = inp
        v_new = ui - jnp.einsum('bhck,bhkv->bhcv', wi, S)
        o = jnp.einsum('bhck,bhkv->bhcv', qi * jnp.exp(gi)[..., None], S) + jnp.einsum('bhcs,bhsv->bhcv', ai, v_new)
        g_last = gi[..., -1]
        k_dec = ki * jnp.exp(g_last[..., None] - gi)[..., None]
        S = S * jnp.exp(g_last)[..., None, None] + jnp.einsum('bhck,bhcv->bhkv', k_dec, v_new)
        return S, o

    S0 = jnp.zeros((B, H, dk, dv), f32)
    _, o = lax.scan(step, S0, xs)
    return jnp.moveaxis(o, 0, 2).reshape(B, H, T, dv).transpose(0, 2, 1, 3)


def gated_deltanet(q, k, v, z, a_in, b_in, conv_w, a_log, dt_bias, norm_g):
    B, T, _ = q.shape
    H, d = DN_HEADS, HEAD_DIM
    qkv = jax.nn.silu(causal_dwconv(jnp.concatenate([q, k, v], axis=-1), conv_w))
    q, k, v = jnp.split(qkv, 3, axis=-1)
    q = l2_norm(q.reshape(B, T, H, d).astype(jnp.float32))
    k = l2_norm(k.reshape(B, T, H, d).astype(jnp.float32))
    v = v.reshape(B, T, H, d)
    g = -jnp.exp(a_log.astype(jnp.float32)) * jax.nn.softplus(a_in.astype(jnp.float32) + dt_bias.astype(jnp.float32))
    beta = jax.nn.sigmoid(b_in.astype(jnp.float32))
    o = chunk_gated_delta_rule(q, k, v, g, beta)
    o = rms_norm(o, norm_g) * jax.nn.silu(z.reshape(B, T, H, d).astype(jnp.float32))
    return o.reshape(B, T, H * d).astype(z.dtype)


def hybrid_layer(x, layer, attn_norm, w_in, sc_conv, lam_q1, lam_k1, lam_q2, lam_k2, diff_norm,
                 dn_conv, dn_a_log, dn_dt_bias, dn_norm, fox_bias, w_out,
                 ffn_norm, w_gate, w_up, ffn_conv, w_down):
    h = rms_norm(x, attn_norm)
    p = h @ w_in
    (sc_h, sc_c, sc_b, df_q, df_k, df_v, dn_q, dn_k, dn_v, dn_z, dn_a, dn_b,
     fx_q, fx_k, fx_v, fx_f) = split_columns(p)
    y_sc = sc_b * causal_dwconv(sc_c * sc_h, sc_conv)
    lam_init = 0.8 - 0.6 * math.exp(-0.3 * layer)
    lam = (jnp.exp(jnp.sum(lam_q1.astype(jnp.float32) * lam_k1.astype(jnp.float32)))
           - jnp.exp(jnp.sum(lam_q2.astype(jnp.float32) * lam_k2.astype(jnp.float32))) + lam_init)
    y_df = diff_attention(df_q, df_k, df_v, lam, lam_init, diff_norm)
    y_dn = gated_deltanet(dn_q, dn_k, dn_v, dn_z, dn_a, dn_b, dn_conv, dn_a_log, dn_dt_bias, dn_norm)
    y_fx = forgetting_attention(fx_q, fx_k, fx_v, fx_f + fox_bias)
    y = jnp.concatenate([y_sc, y_df, y_dn, y_fx], axis=-1) @ w_out
    x = x + y
    h = rms_norm(x, ffn_norm)
    a = causal_dwconv(h @ w_gate, ffn_conv)
    x = x + (jax.nn.silu(a) * (h @ w_up)) @ w_down
    return x


def setup_inputs(seed: int = 0) -> dict:
    key = jax.random.key(seed)
    ks = jax.random.split(key, 24)
    f32 = jnp.float32
    L = DEPTH

    def normal(k, shape, scale):
        return jax.random.normal(k, shape, f32) * scale

    def gain(k, shape):
        return 1.0 + 0.02 * jax.random.normal(k, shape, f32)

    dt = jnp.exp(jax.random.uniform(ks[12], (L, DN_HEADS), f32, math.log(1e-3), math.log(1e-1)))
    return {
        "x": jax.random.normal(ks[0], (BATCH, SEQ, D_MODEL), f32),
        "attn_norm": gain(ks[1], (L, D_MODEL)),
        "w_in": normal(ks[2], (L, D_MODEL, D_IN), D_MODEL ** -0.5),
        "sc_conv": normal(ks[3], (L, SC_KERNEL, GROUP_WIDTH), SC_KERNEL ** -0.5),
        "lam_q1": normal(ks[4], (L, HEAD_DIM), 0.1),
        "lam_k1": normal(ks[5], (L, HEAD_DIM), 0.1),
        "lam_q2": normal(ks[6], (L, HEAD_DIM), 0.1),
        "lam_k2": normal(ks[7], (L, HEAD_DIM), 0.1),
        "diff_norm": gain(ks[8], (L, 2 * HEAD_DIM)),
        "dn_conv": normal(ks[9], (L, DN_CONV, 3 * GROUP_WIDTH), DN_CONV ** -0.5),
        "dn_a_log": jnp.log(jax.random.uniform(ks[10], (L, DN_HEADS), f32, 1.0, 16.0)),
        "dn_dt_bias": dt + jnp.log(-jnp.expm1(-dt)),
        "dn_norm": gain(ks[11], (L, HEAD_DIM)),
        "fox_bias": 2.0 + 0.1 * jax.random.normal(ks[13], (L, FOX_HEADS), f32),
        "w_out": normal(ks[14], (L, D_MIX, D_MODEL), D_MIX ** -0.5),
        "ffn_norm": gain(ks[15], (L, D_MODEL)),
        "w_gate": normal(ks[16], (L, D_MODEL, D_FF), D_MODEL ** -0.5),
        "w_up": normal(ks[17], (L, D_MODEL, D_FF), D_MODEL ** -0.5),
        "ffn_conv": normal(ks[18], (L, FFN_KERNEL, D_FF), FFN_KERNEL ** -0.5),
        "w_down": normal(ks[19], (L, D_FF, D_MODEL), D_FF ** -0.5),
        "final_norm": gain(ks[20], (D_MODEL,)),
    }


def reference(x, attn_norm, w_in, sc_conv, lam_q1, lam_k1, lam_q2, lam_k2, diff_norm,
              dn_conv, dn_a_log, dn_dt_bias, dn_norm, fox_bias, w_out,
              ffn_norm, w_gate, w_up, ffn_conv, w_down, final_norm):
    for l in range(DEPTH):
        x = hybrid_layer(x, l, attn_norm[l], w_in[l], sc_conv[l], lam_q1[l], lam_k1[l], lam_q2[l], lam_k2[l],
                         diff_norm[l], dn_conv[l], dn_a_log[l], dn_dt_bias[l], dn_norm[l], fox_bias[l], w_out[l],
                         ffn_norm[l], w_gate[l], w_up[l], ffn_conv[l], w_down[l])
    return rms_norm(x, final_norm)
```

```python
import contextlib
import numpy as np
import concourse.bass as bass
import concourse.mybir as mybir

F32 = mybir.dt.float32
BF16 = mybir.dt.bfloat16
AF = mybir.ActivationFunctionType
ALU = mybir.AluOpType
AX = mybir.AxisListType
ENGS = ("pe", "act", "dve", "pool", "sp")
EPS = 1e-6


class H:
    __slots__ = ("name", "last_w", "readers")

    def __init__(self, name=""):
        self.name = name
        self.last_w = None
        self.readers = []


class HX(H):
    __slots__ = ()


class DSem:
    def __init__(self, sem):
        self.sem = sem
        self.count = 0


class Sched:
    def __init__(self, nc):
        self.nc = nc
        self.stack = contextlib.ExitStack()
        self.streams = {e: [] for e in ENGS}
        self.esem = {}
        self.ecount = {e: 0 for e in ENGS}
        self.waited = {}
        self.dsems = []
        for e in ENGS:
            self.esem[e] = self.stack.enter_context(nc.semaphore("es_" + e))
        self.n_ops = 0
        self.n_waits = 0
        self.uid = 0

    def sbuf(self, name, shape, dtype, stack=None):
        self.uid += 1
        return (stack or self.stack).enter_context(
            self.nc.sbuf_tensor(f"{name}_{self.uid}", list(shape), dtype))

    def psum(self, name, shape, dtype=F32, stack=None):
        self.uid += 1
        return (stack or self.stack).enter_context(
            self.nc.psum_tensor(f"{name}_{self.uid}", list(shape), dtype))

    def dsem(self, name):
        self.uid += 1
        d = DSem(self.stack.enter_context(self.nc.semaphore(f"{name}_{self.uid}")))
        self.dsems.append(d)
        return d

    def _deps(self, eng, reads, writes):
        deps = {}

        def add(tok):
            if tok is None:
                return
            k, v = tok
            if k[0] == "d":
                v = k[1].count
            if deps.get(k, 0) < v:
                deps[k] = v
        raw_same = 0
        for h in reads:
            add(h.last_w)
            if h.last_w is not None and h.last_w[0] == ("e", eng) and eng != "pe":
                raw_same = max(raw_same, h.last_w[1])
        for h in writes:
            add(h.last_w)
            for r in h.readers:
                add(r)
        out = []
        for k, v in deps.items():
            if k == ("e", eng):
                if raw_same == 0:
                    continue
                v = raw_same
            wk = (eng, k)
            if self.waited.get(wk, 0) >= v:
                continue
            self.waited[wk] = v
            out.append((k, v))
        return out

    def _sem_of(self, k):
        if k[0] == "e":
            return self.esem[k[1]]
        return k[1].sem

    def _post(self, tok, reads, writes):
        for h in reads:
            h.readers.append(tok)
            if len(h.readers) > 48:
                mx = {}
                for k, v in h.readers:
                    if mx.get(k, 0) < v:
                        mx[k] = v
                h.readers = list(mx.items())
        for h in writes:
            h.last_w = tok
            h.readers = []

    def op(self, eng, name, *args, reads=(), writes=(), **kwargs):
        if any(isinstance(h, HX) for h in reads):
            writes = list(writes) + [h for h in reads if isinstance(h, HX)]
            reads = [h for h in reads if not isinstance(h, HX)]
        fn = (lambda e, name=name, args=args, kwargs=kwargs: getattr(e, name)(*args, **kwargs))
        for k, v in self._deps(eng, reads, writes):
            self.streams[eng].append(("w", self._sem_of(k), v))
            self.n_waits += 1
        self.ecount[eng] += 1
        self.streams[eng].append(("o", fn))
        tok = (("e", eng), self.ecount[eng])
        self._post(tok, reads, writes)
        self.n_ops += 1
        return tok

    def dma(self, q, dsem, out, in_, reads=(), writes=(), **kwargs):
        fn = (lambda e, out=out, in_=in_, kwargs=kwargs: e.dma_start(out=out, in_=in_, **kwargs))
        for k, v in self._deps(q, reads, writes):
            self.streams[q].append(("w", self._sem_of(k), v))
            self.n_waits += 1
        dsem.count += 16
        self.streams[q].append(("d", fn, dsem.sem))
        tok = (("d", dsem), dsem.count)
        self._post(tok, reads, writes)
        self.n_ops += 1
        return tok

    def barrier(self):
        for e in ENGS:
            for o in ENGS:
                if o == e or self.ecount[o] == 0:
                    continue
                wk = (e, ("e", o))
                if self.waited.get(wk, 0) < self.ecount[o]:
                    self.waited[wk] = self.ecount[o]
                    self.streams[e].append(("w", self.esem[o], self.ecount[o]))
            for d in self.dsems:
                if d.count == 0:
                    continue
                wk = (e, ("d", d))
                if self.waited.get(wk, 0) < d.count:
                    self.waited[wk] = d.count
                    self.streams[e].append(("w", d.sem, d.count))

    def emit(self):
        nc = self.nc
        with nc.Block() as block:
            def mk(eng):
                def body(e):
                    sem = self.esem[eng]
                    for it in self.streams[eng]:
                        if it[0] == "w":
                            e.wait_ge(it[1], it[2])
                        elif it[0] == "o":
                            it[1](e).then_inc(sem, 1)
                        else:
                            it[1](e).then_inc(it[2], 16)
                return body
            block.tensor(mk("pe"))
            block.scalar(mk("act"))
            block.vector(mk("dve"))
            block.gpsimd(mk("pool"))
            block.sync(mk("sp"))

    def close(self):
        self.stack.close()


class WStream:
    def __init__(self, s, nslots, kcp=8, width=512, q="pool", dtype=BF16):
        self.s = s
        self.kcp = kcp
        self.width = width
        self.q = q
        self.slots = [s.sbuf("wring", [128, kcp, width], dtype) for _ in range(nslots)]
        self.h = [H() for _ in range(nslots)]
        self.sem = [s.dsem("wr") for _ in range(nslots)]
        self.plan = []
        self.issued = 0
        self.consumed = 0

    def add(self, w2d, kc0, nkc, c0, width, tag=None):
        src = w2d[kc0 * 128:(kc0 + nkc) * 128, c0:c0 + width].rearrange("(c p) n -> p c n", p=128)
        self.plan.append((src, nkc, width, tag))

    def _issue(self):
        i = self.issued
        src, nkc, width, tag = self.plan[i]
        sl = i % len(self.slots)
        self.s.dma(self.q, self.sem[sl], self.slots[sl][:, 0:nkc, 0:width], src, writes=[self.h[sl]])
        self.issued += 1

    def start(self):
        while self.issued < len(self.plan) and self.issued < len(self.slots):
            self._issue()

    def get(self, tag=None):
        i = self.consumed
        assert i < self.issued, "piece not issued"
        src, nkc, width, ptag = self.plan[i]
        assert tag is None or tag == ptag, (tag, ptag)
        sl = i % len(self.slots)
        return self.slots[sl], self.h[sl], nkc, width

    def done(self):
        self.consumed += 1
        if self.issued < len(self.plan):
            self._issue()


def make_identity(s, dtype):
    ident = s.sbuf("ident", [128, 128], dtype)
    hid = H()
    s.op("pool", "memset", ident[:], 0.0, writes=[hid])
    s.op("pool", "affine_select", out=ident[:], in_=ident[:], pattern=[[-1, 128]],
         compare_op=ALU.not_equal, fill=1.0, base=0, channel_multiplier=1, reads=[hid], writes=[hid])
    return ident, hid


def norm_transpose(s, cfg, x_blk, hx, ntok, hn, hhn, rs_tiles, ident_b, hid, pst, hpst, hT, hhT, tcol0, gcol, hg):
    D = cfg["D"]
    KC = D // 128
    ss, rstd, hrs = rs_tiles
    s.op("act", "activation", out=hn[0:ntok, :], in_=x_blk, func=AF.Square, accum_out=ss[0:ntok, :],
         reads=[hx], writes=[hhn, hrs])
    s.op("act", "activation", out=rstd[0:ntok, :], in_=ss[0:ntok, :], func=AF.Sqrt, scale=1.0 / D, bias=EPS,
         reads=[hrs], writes=[hrs])
    s.op("dve", "reciprocal", out=rstd[0:ntok, :], in_=rstd[0:ntok, :], reads=[hrs], writes=[hrs])
    s.op("act", "activation", out=hn[0:ntok, :], in_=x_blk, func=AF.Copy, scale=rstd[0:ntok, :],
         reads=[hx, hrs], writes=[hhn])
    GS = min(8, KC)
    for g in range(KC // GS):
        p = pst[g % 2]
        hp = hpst[g % 2]
        for j in range(GS):
            kc = g * GS + j
            s.op("pe", "transpose", p[:, j, 0:ntok], hn[0:ntok, kc * 128:(kc + 1) * 128], ident_b[0:ntok, 0:ntok],
                 reads=[hhn, hid], writes=[hp])
        eng = "dve" if g % 2 == 0 else "act"
        if eng == "dve":
            s.op("dve", "tensor_tensor", out=hT[:, g * GS:(g + 1) * GS, tcol0:tcol0 + ntok], in0=p[:, 0:GS, 0:ntok],
                 in1=gcol[:, g * GS:(g + 1) * GS].unsqueeze(2).to_broadcast([128, GS, ntok]), op=ALU.mult,
                 reads=[hp, hg], writes=[hhT])
        else:
            s.op("act", "activation", out=hT[:, g * GS:(g + 1) * GS, tcol0:tcol0 + ntok], in_=p[:, 0:GS, 0:ntok],
                 func=AF.Copy, reads=[hp], writes=[hhT])
            s.op("dve", "tensor_tensor", out=hT[:, g * GS:(g + 1) * GS, tcol0:tcol0 + ntok],
                 in0=hT[:, g * GS:(g + 1) * GS, tcol0:tcol0 + ntok],
                 in1=gcol[:, g * GS:(g + 1) * GS].unsqueeze(2).to_broadcast([128, GS, ntok]), op=ALU.mult,
                 reads=[hhT, hg], writes=[hhT])


NFM = 22
N32 = 14
NSMALL = 6
FMW = NFM * 128 + NSMALL
TMW = 512
WGW = FMW + TMW
R32 = N32 * 128 + NSMALL
R16 = (NFM - N32) * 128


def fm_tiles():
    tiles = []
    for t in range(5):
        tiles.append((t * 512, 512, [t * 4 + i for i in range(4)], False))
    tiles.append((2560, 262, [20, 21], True))
    return tiles


def phase_P1(s, cfg, io, ws, plan_only, st=None):
    D, T, TT = cfg["D"], cfg["T"], cfg["TT1"]
    KC = D // 128
    NTB = TT // 128
    if not plan_only:
        xs, hxs = st["xs"], st["hxs"]
        hn, hhn = st["hn"], st["hhn"]
        hTs, hhTs = st["hT"], st["hhT"]
        ps, hps = st["ps"], st["hps"]
        pst = [ps[7][:].bitcast(BF16).rearrange("p (a b) -> p a b", b=128)] * 2
        hpst = [hps[7]] * 2
        s32, h32, s16, h16 = st["s32"], st["h32"], st["s16"], st["h16"]
        gcol, hgcol = st["gcol"], st["hgcol"]
        ident, hid = st["ident"], st["hid"]
        rs = st["rs"]
        dl, dst_ = st["dl"], st["dst"]
        hscr = st["hscr"]
    blkctr = 0
    evc = 0
    xbc = 0
    ntiles = T // TT

    def load_tile(ti):
        for tb in range(NTB):
            r0 = ti * TT + tb * 128
            s.dma("sp", st["dlx"][tb], xs[tb][:], io["xb"][r0:r0 + 128, :], writes=[hxs[tb]])

    def norm_tb(ti, tb):
        norm_transpose(s, cfg, xs[tb][:], hxs[tb], 128, hn, hhn, rs, ident, hid, pst, hpst,
                       hTs[ti % 2], hhTs[ti % 2], tb * 128, gcol, hgcol)

    if not plan_only:
        load_tile(0)
        for tb in range(NTB):
            norm_tb(0, tb)
    for ti in range(ntiles):
        if not plan_only:
            hT, hhT = hTs[ti % 2], hhTs[ti % 2]
            if ti + 1 < ntiles:
                load_tile(ti + 1)
        for (c0, w, blks, has_small) in fm_tiles():
            banks = [(blkctr + i) % 7 for i in range(len(blks) + (1 if has_small else 0))]
            blkctr += len(banks)
            for kp in range(0, KC, 8):
                nk = min(8, KC - kp)
                if plan_only:
                    ws.add(io["wg"], kp, nk, c0, w, "fm")
                    continue
                slot, hs, nkc, width = ws.get("fm")
                for kcl in range(nkc):
                    kc = kp + kcl
                    for i, b in enumerate(blks):
                        s.op("pe", "matmul", ps[banks[i]][:, 0:TT], lhsT=slot[:, kcl, i * 128:(i + 1) * 128],
                             rhs=hT[:, kc, 0:TT], start=(kc == 0), stop=(kc == KC - 1),
                             reads=[hhT, hs], writes=[hps[banks[i]]])
                    if has_small:
                        i = len(blks)
                        s.op("pe", "matmul", ps[banks[i]][0:NSMALL, 0:TT], lhsT=slot[:, kcl, i * 128:i * 128 + NSMALL],
                             rhs=hT[:, kc, 0:TT], start=(kc == 0), stop=(kc == KC - 1),
                             reads=[hhT, hs], writes=[hps[banks[i]]])
                ws.done()
            if plan_only:
                continue
            items = [(b, 128) for b in blks] + ([(-1, NSMALL)] if has_small else [])
            for i, (b, np_) in enumerate(items):
                bank = banks[i]
                is32 = (b < N32)
                k = evc % 3
                evc += 1
                stg, hst = (s32[k], h32[k]) if is32 else (s16[k], h16[k])
                dsk = st["d32"][k] if is32 else st["d16"][k]
                if evc % 2 == 0:
                    s.op("act", "activation", out=stg[0:np_, 0:TT], in_=ps[bank][0:np_, 0:TT], func=AF.Copy,
                         reads=[hps[bank]], writes=[hst])
                else:
                    s.op("dve", "tensor_copy", out=stg[0:np_, 0:TT], in_=ps[bank][0:np_, 0:TT],
                         reads=[hps[bank]], writes=[hst])
                if b == -1:
                    dstap = io["scr32"][N32 * 128:N32 * 128 + NSMALL, ti * TT:(ti + 1) * TT]
                elif is32:
                    dstap = io["scr32"][b * 128:(b + 1) * 128, ti * TT:(ti + 1) * TT]
                else:
                    dstap = io["scr16"][(b - N32) * 128:(b - N32 + 1) * 128, ti * TT:(ti + 1) * TT]
                s.dma("sp", dsk, dstap, stg[0:np_, 0:TT], reads=[hst], writes=[hscr])
            fti = c0 // 512
            if ti + 1 < ntiles and fti < NTB:
                norm_tb(ti + 1, fti)
        banks = [(blkctr + i) % 7 for i in range(NTB)]
        blkctr += NTB
        for kp in range(0, KC, 8):
            nk = min(8, KC - kp)
            if plan_only:
                ws.add(io["wg"], kp, nk, FMW, TMW, "tm")
                continue
            slot, hs, nkc, width = ws.get("tm")
            for kcl in range(nkc):
                kc = kp + kcl
                for tb in range(NTB):
                    s.op("pe", "matmul", ps[banks[tb]][:, 0:TMW], lhsT=hT[:, kc, tb * 128:(tb + 1) * 128],
                         rhs=slot[:, kcl, 0:TMW], start=(kc == 0), stop=(kc == KC - 1),
                         reads=[hhT, hs], writes=[hps[banks[tb]]])
            ws.done()
        if plan_only:
            continue
        for tb in range(NTB):
            bank = banks[tb]
            k = evc % 3
            evc += 1
            stg, hst = s16[k], h16[k]
            if evc % 2 == 0:
                s.op("act", "activation", out=stg[:, 0:TMW], in_=ps[bank][:, 0:TMW], func=AF.Copy,
                     reads=[hps[bank]], writes=[hst])
            else:
                s.op("dve", "tensor_copy", out=stg[:, 0:TMW], in_=ps[bank][:, 0:TMW], reads=[hps[bank]], writes=[hst])
            r0 = ti * TT + tb * 128
            s.dma("sp", st["d16"][k], io["vtm"][r0:r0 + 128, :], stg[:, 0:TMW], reads=[hst], writes=[hscr])


def setup_P1(s, cfg, io, stack):
    D, T, TT = cfg["D"], cfg["T"], cfg["TT1"]
    KC = D // 128
    st = {}
    st["xs"] = [s.sbuf("xs", [128, D], F32, stack) for _ in range(TT // 128)]
    st["hxs"] = [H() for _ in range(TT // 128)]
    st["hn"] = s.sbuf("hn", [128, D], BF16, stack)
    st["hhn"] = H()
    st["hT"] = [s.sbuf("hT", [128, KC, TT], BF16, stack) for _ in range(2)]
    st["hhT"] = [H(), H()]
    st["ps"] = [s.psum("ps", [128, 512], F32, stack) for _ in range(8)]
    st["hps"] = [H() for _ in range(8)]
    st["s32"] = [s.sbuf("s32", [128, 512], F32, stack) for _ in range(3)]
    st["h32"] = [H() for _ in range(3)]
    st["s16"] = [s.sbuf("s16", [128, 512], BF16, stack) for _ in range(3)]
    st["h16"] = [H() for _ in range(3)]
    st["gcol"] = s.sbuf("gcol", [128, KC], F32, stack)
    st["hgcol"] = H()
    st["rs"] = (s.sbuf("ss", [128, 1], F32, stack), s.sbuf("rstd", [128, 1], F32, stack), H())
    st["dl"] = s.dsem("dl")
    st["dlx"] = [s.dsem("dlx") for _ in range(TT // 128)]
    st["d32"] = [s.dsem("d32") for _ in range(3)]
    st["d16"] = [s.dsem("d16") for _ in range(3)]
    st["dst"] = s.dsem("dst")
    st["hscr"] = io["hscr"]
    st["ident"], st["hid"] = io["ident_b"], io["hid_b"]
    s.dma("sp", st["dl"], st["gcol"][:], io["attn_norm"].rearrange("(c p) -> p c", p=128), writes=[st["hgcol"]],
          allow_slow_non_contiguous=True)
    return st


SCALE = 128 ** -0.5


def make_masks(s, stack):
    mk = s.sbuf("cmask", [128, 4, 512], BF16, stack)
    mf = s.sbuf("cmaskf", [128, 512], F32, stack)
    hm = H()
    hmf = H()
    for j in range(4):
        s.op("pool", "memset", mf[:], 1.0, writes=[hmf])
        s.op("pool", "affine_select", out=mf[:], in_=mf[:], pattern=[[1, 512]], compare_op=ALU.is_ge, fill=0.0,
             base=-128 * j, channel_multiplier=-1, reads=[hmf], writes=[hmf])
        s.op("pool", "tensor_copy", out=mk[:, j, :], in_=mf[:], reads=[hmf], writes=[hm])
    return mk, hm


def phase_SC(s, cfg, io, stack):
    T = cfg["T"]
    TS = min(2048, T)
    cw = s.sbuf("sccw", [128, 2, 3], F32, stack)
    hcw = H()
    dc = s.dsem("dc")
    for j in range(3):
        s.dma("sp", dc, cw[:, :, j], io["sc_conv"][j].rearrange("(c p) -> p c", p=128), writes=[hcw],
              allow_slow_non_contiguous=True)
    NB = 2
    bufs = []
    for i in range(NB):
        bufs.append(dict(
            hin=s.sbuf("sc_hin", [128, TS + 2], F32, stack), c=s.sbuf("sc_c", [128, TS + 2], F32, stack),
            b=s.sbuf("sc_b", [128, TS], F32, stack), t=s.sbuf("sc_t", [128, TS], F32, stack),
            y=s.sbuf("sc_y", [128, TS], BF16, stack),
            h=[H() for _ in range(5)], d=[s.dsem("scd") for _ in range(4)]))
    it = 0
    for blk in range(2):
        for t0 in range(0, T, TS):
            B = bufs[it % NB]
            it += 1
            hin, c, b, t, y = B["hin"], B["c"], B["b"], B["t"], B["y"]
            hh, hc, hb, ht, hy = B["h"]
            rh, rc, rb = blk * 128, (2 + blk) * 128, (4 + blk) * 128
            if t0 == 0:
                s.op("pool", "memset", hin[:, 0:2], 0.0, writes=[hh])
                s.op("pool", "memset", c[:, 0:2], 0.0, writes=[hc])
                s.dma("sp", B["d"][0], hin[:, 2:], io["scr32"][rh:rh + 128, 0:TS], reads=[io["hscr"]], writes=[hh])
                s.dma("sp", B["d"][1], c[:, 2:], io["scr32"][rc:rc + 128, 0:TS], reads=[io["hscr"]], writes=[hc])
            else:
                s.dma("sp", B["d"][0], hin[:], io["scr32"][rh:rh + 128, t0 - 2:t0 + TS], reads=[io["hscr"]], writes=[hh])
                s.dma("sp", B["d"][1], c[:], io["scr32"][rc:rc + 128, t0 - 2:t0 + TS], reads=[io["hscr"]], writes=[hc])
            s.dma("sp", B["d"][2], b[:], io["scr32"][rb:rb + 128, t0:t0 + TS], reads=[io["hscr"]], writes=[hb])
            s.op("pool", "tensor_tensor", out=hin[:], in0=hin[:], in1=c[:], op=ALU.mult, reads=[hh, hc], writes=[hh])
            s.op("dve", "tensor_scalar", out=t[:], in0=hin[:, 2:TS + 2], scalar1=cw[:, blk, 2:3], scalar2=None,
                 op0=ALU.mult, reads=[hh, hcw], writes=[ht])
            s.op("dve", "scalar_tensor_tensor", out=t[:], in0=hin[:, 1:TS + 1], scalar=cw[:, blk, 1:2], in1=t[:],
                 op0=ALU.mult, op1=ALU.add, reads=[hh, hcw, ht], writes=[ht])
            s.op("dve", "scalar_tensor_tensor", out=t[:], in0=hin[:, 0:TS], scalar=cw[:, blk, 0:1], in1=t[:],
                 op0=ALU.mult, op1=ALU.add, reads=[hh, hcw, ht], writes=[ht])
            s.op("pool", "tensor_tensor", out=y[:], in0=t[:], in1=b[:], op=ALU.mult, reads=[ht, hb], writes=[hy])
            s.dma("sp", B["d"][3], io["ymT"][blk * 128:(blk + 1) * 128, t0:t0 + TS], y[:], reads=[hy],
                  writes=[io["hym"]])


def attn_core(s, cfg, QT, KT, hK, QTs, hQ, V, hV, ndv, bias_fn, mk, hm, ones_b, hones, ps, hps, pts, hpts, finish):
    for qt in range(cfg["T"] // QT):
        attn_qt(s, cfg, QT, qt, KT, hK, QTs, hQ, V, hV, ndv, bias_fn, mk, hm, ones_b, hones, ps, pts, hpts, finish)


def phase_FOX(s, cfg, io, stack, mk, hm, ones_b, hones, ones_f, honesf, tri, htri):
    T = cfg["T"]
    QT = 256 if T >= 256 else T
    NKB = T // 128
    NQT = T // QT
    S = [s.psum("fxS", [128, 512], F32, stack) for _ in range(2)]
    O = [s.psum("fxO", [128, 512], F32, stack)]
    L = s.psum("fxL", [128, 512], F32, stack)
    pc = s.psum("fxC", [128, 512], F32, stack)
    hpc = H()
    ps = dict(S=S, hS=[H(), H()], O=O, hO=[H()], L=L, hL=H(), hbias=H())
    pts = [s.sbuf("fxpt", [128, QT], BF16, stack) for _ in range(3)]
    hpts = [H() for _ in range(3)]
    for h in range(2):
        KT = s.sbuf("fxK", [128, T], BF16, stack)
        QTs = s.sbuf("fxQ", [128, T], BF16, stack)
        V = s.sbuf("fxV", [128, NKB, 128], BF16, stack)
        yo = s.sbuf("fxy", [128, T], BF16, stack)
        btab = s.sbuf("fxbt", [128, NKB, NQT], F32, stack)
        fcol = s.sbuf("fxf", [128, NKB], F32, stack)
        lcol = s.sbuf("fxl", [128, NKB], F32, stack)
        binc = s.sbuf("fxbi", [128, NKB], F32, stack)
        ccol = s.sbuf("fxc", [128, NKB], F32, stack)
        onesk = s.sbuf("fxo", [128, NKB], F32, stack)
        nfb = s.sbuf("fxnfb", [128, 1], F32, stack)
        rl = s.sbuf("fxrl", [128, QT], F32, stack)
        hK, hQ, hV, hy, hf, hrl = H(), H(), H(), H(), H(), H()
        d = [s.dsem("fxd") for _ in range(6)]
        s.dma("sp", d[0], KT[:], io["scr16"][768 + 128 * h:896 + 128 * h, :], reads=[io["hscr"]], writes=[hK])
        s.dma("sp", d[1], QTs[:], io["scr16"][512 + 128 * h:640 + 128 * h, :], reads=[io["hscr"]], writes=[hQ])
        s.dma("sp", d[2], V[:], io["vtm"][:, 256 + 128 * h:384 + 128 * h].rearrange("(kb p) c -> p kb c", p=128),
              reads=[io["hscr"]], writes=[hV])
        r = N32 * 128 + 4 + h
        s.dma("sp", d[3], fcol[:], io["scr32"][r, :].rearrange("(kb p) -> p kb", p=128), reads=[io["hscr"]],
              writes=[hf], allow_slow_non_contiguous=True)
        s.dma("sp", d[4], nfb[:], io["fox_bias"][h:h + 1].partition_broadcast(128), writes=[hf])
        s.op("dve", "tensor_scalar", out=nfb[:], in0=nfb[:], scalar1=-1.0, scalar2=None, op0=ALU.mult,
             reads=[hf], writes=[hf])
        s.op("pool", "memset", onesk[:], 1.0, writes=[hf])
        s.op("act", "activation", out=lcol[:], in_=fcol[:], func=AF.Exp, scale=-1.0, bias=nfb[:, 0:1],
             reads=[hf], writes=[hf])
        s.op("act", "activation", out=lcol[:], in_=lcol[:], func=AF.Ln, scale=1.0, bias=1.0, reads=[hf], writes=[hf])
        s.op("pe", "matmul", pc[:, 0:NKB], lhsT=tri[:], rhs=lcol[:], start=True, stop=True,
             reads=[htri, hf], writes=[hpc])
        s.op("pe", "matmul", pc[:, NKB:2 * NKB], lhsT=ones_f[:], rhs=lcol[:], start=True, stop=True,
             reads=[honesf, hf], writes=[hpc])
        s.op("dve", "tensor_tensor_scan", out=binc[:], data0=onesk[:], data1=pc[:, NKB:2 * NKB], initial=0.0,
             op0=ALU.mult, op1=ALU.add, reads=[hpc, hf], writes=[hf])
        s.op("dve", "tensor_tensor", out=ccol[:], in0=pc[:, 0:NKB], in1=binc[:], op=ALU.add, reads=[hpc, hf], writes=[hf])
        s.op("dve", "tensor_tensor", out=ccol[:], in0=ccol[:], in1=pc[:, NKB:2 * NKB], op=ALU.subtract,
             reads=[hpc, hf], writes=[hf])
        QB = QT // 128
        kbm0 = max(QB // 2 - 1, 0)
        if QB >= 1 and NQT > 1:
            bref = binc[:, kbm0:kbm0 + (NQT - 1) * QB + 1:QB]
        else:
            bref = binc[:, kbm0:kbm0 + 1]
        for kb in range(NKB):
            s.op("dve", "tensor_scalar", out=btab[:, kb, :], in0=bref, scalar1=ccol[:, kb:kb + 1], scalar2=-1.0,
                 op0=ALU.subtract, op1=ALU.mult, reads=[hf], writes=[ps["hbias"]])

        def bias_fn(kb, qt, btab=btab):
            return btab[:, kb, qt:qt + 1]

        def finish(qt, yo=yo, hy=hy, rl=rl, hrl=hrl):
            s.op("dve", "reciprocal", out=rl[:, 0:QT], in_=L[:, 0:QT], reads=[ps["hL"]], writes=[hrl])
            s.op("dve", "tensor_tensor", out=yo[:, qt * QT:(qt + 1) * QT], in0=O[0][:, 0:QT], in1=rl[:, 0:QT],
                 op=ALU.mult, reads=[ps["hO"][0], hrl], writes=[hy])
        attn_core(s, cfg, QT, KT, hK, QTs, hQ, V, hV, 1, bias_fn, mk, hm, ones_b, hones, ps, hps=None,
                  pts=pts, hpts=hpts, finish=finish)
        s.dma("sp", d[5], io["ymT"][768 + 128 * h:896 + 128 * h, :], yo[:], reads=[hy], writes=[io["hym"]])


def phase_DIFF(s, cfg, io, stack, mk, hm, ones_b, hones, ones_f, honesf, lam_init):
    T = cfg["T"]
    QT = 512 if T >= 512 else T
    NKB = T // 128
    S = [s.psum("dfS", [128, 512], F32, stack) for _ in range(2)]
    Om = [[s.psum("dfO", [128, 512], F32, stack) for _ in range(2)] for _ in range(2)]
    Lm = [s.psum("dfL", [128, 512], F32, stack) for _ in range(2)]
    hS = [H(), H()]
    hOm = [[H(), H()], [H(), H()]]
    hLm = [H(), H()]
    pts = [s.sbuf("dfpt", [128, QT], BF16, stack) for _ in range(3)]
    hpts = [H() for _ in range(3)]
    KT = [s.sbuf("dfK", [128, T], BF16, stack) for _ in range(2)]
    QTs = [s.sbuf("dfQ", [128, T], BF16, stack) for _ in range(2)]
    V = s.sbuf("dfV", [128, NKB, 256], BF16, stack)
    yo = s.sbuf("dfy", [128, 2, T], BF16, stack)
    hK, hQ, hV, hy = [H(), H()], [H(), H()], H(), H()
    d = [s.dsem("dfd") for _ in range(12)]
    for m in range(2):
        s.dma("sp", d[m], KT[m][:], io["scr16"][256 + 128 * m:384 + 128 * m, :], reads=[io["hscr"]], writes=[hK[m]])
        s.dma("sp", d[2 + m], QTs[m][:], io["scr16"][128 * m:128 + 128 * m, :], reads=[io["hscr"]], writes=[hQ[m]])
    s.dma("sp", d[4], V[:], io["vtm"][:, 0:256].rearrange("(kb p) c -> p kb c", p=128), reads=[io["hscr"]], writes=[hV])
    lv = s.sbuf("dflv", [128, 4, 128], F32, stack)
    lt = s.sbuf("dflt", [128, 2, 128], F32, stack)
    lsc = s.sbuf("dfls", [128, 4], F32, stack)
    gsc = s.sbuf("dfg", [128, 2], F32, stack)
    hl = H()
    for i, nm in enumerate(["lam_q1", "lam_k1", "lam_q2", "lam_k2"]):
        s.dma("sp", d[5 + i], lv[:, i, :], io[nm].partition_broadcast(128), writes=[hl])
    s.dma("sp", d[9], gsc[:], io["diff_norm"].rearrange("(c p) -> p c", p=128), writes=[hl], allow_slow_non_contiguous=True)
    s.op("dve", "tensor_tensor", out=lt[:, 0, :], in0=lv[:, 0, :], in1=lv[:, 1, :], op=ALU.mult, reads=[hl], writes=[hl])
    s.op("dve", "tensor_tensor", out=lt[:, 1, :], in0=lv[:, 2, :], in1=lv[:, 3, :], op=ALU.mult, reads=[hl], writes=[hl])
    s.op("dve", "tensor_reduce", out=lsc[:, 0:2], in_=lt[:], axis=AX.X, op=ALU.add, reads=[hl], writes=[hl])
    s.op("act", "activation", out=lsc[:, 0:2], in_=lsc[:, 0:2], func=AF.Exp, reads=[hl], writes=[hl])
    s.op("dve", "tensor_tensor", out=lsc[:, 2:3], in0=lsc[:, 1:2], in1=lsc[:, 0:1], op=ALU.subtract, reads=[hl], writes=[hl])
    s.op("dve", "tensor_scalar", out=lsc[:, 2:3], in0=lsc[:, 2:3], scalar1=-float(lam_init), scalar2=None, op0=ALU.add,
         reads=[hl], writes=[hl])
    s.op("dve", "tensor_scalar", out=gsc[:], in0=gsc[:], scalar1=float(1.0 - lam_init), scalar2=None, op0=ALU.mult,
         reads=[hl], writes=[hl])
    r0 = s.sbuf("dfr0", [128, QT], F32, stack)
    r1 = s.sbuf("dfr1", [128, QT], F32, stack)
    dd = s.sbuf("dfd", [128, 2, QT], F32, stack)
    tmp = s.sbuf("dftmp", [128, QT], F32, stack)
    sq = s.sbuf("dfsq", [128, 2, QT], F32, stack)
    rstd = s.sbuf("dfrs", [128, QT], F32, stack)
    hc = H()
    NQT = T // QT

    def make_finish(m):
        def finish(qt):
            if m == 0:
                return
            s.op("dve", "reciprocal", out=r0[:, 0:QT], in_=Lm[0][:, 0:QT], reads=[hLm[0]], writes=[hc])
            s.op("dve", "reciprocal", out=r1[:, 0:QT], in_=Lm[1][:, 0:QT], reads=[hLm[1]], writes=[hc])
            s.op("dve", "tensor_scalar", out=r1[:, 0:QT], in0=r1[:, 0:QT], scalar1=lsc[:, 2:3], scalar2=None,
                 op0=ALU.mult, reads=[hc, hl], writes=[hc])
            for hf_ in range(2):
                s.op("dve", "tensor_tensor", out=dd[:, hf_, :], in0=Om[0][hf_][:, 0:QT], in1=r0[:, 0:QT], op=ALU.mult,
                     reads=[hOm[0][hf_], hc], writes=[hc])
                s.op("dve", "tensor_tensor", out=tmp[:, 0:QT], in0=Om[1][hf_][:, 0:QT], in1=r1[:, 0:QT], op=ALU.mult,
                     reads=[hOm[1][hf_], hc], writes=[hc])
                s.op("dve", "tensor_tensor", out=dd[:, hf_, :], in0=dd[:, hf_, :], in1=tmp[:, 0:QT], op=ALU.add,
                     reads=[hc], writes=[hc])
                s.op("act", "activation", out=sq[:, hf_, :], in_=dd[:, hf_, :], func=AF.Square, reads=[hc], writes=[hc])
            for hf_ in range(2):
                s.op("pe", "matmul", S[0][:, 0:QT], lhsT=ones_f[:], rhs=sq[:, hf_, :], start=(hf_ == 0), stop=(hf_ == 1),
                     reads=[honesf, hc], writes=[hS[0]])
            s.op("act", "activation", out=rstd[:, 0:QT], in_=S[0][:, 0:QT], func=AF.Sqrt, scale=1.0 / 256, bias=EPS,
                 reads=[hS[0]], writes=[hc])
            s.op("dve", "reciprocal", out=rstd[:, 0:QT], in_=rstd[:, 0:QT], reads=[hc], writes=[hc])
            for hf_ in range(2):
                s.op("dve", "scalar_tensor_tensor", out=yo[:, hf_, qt * QT:(qt + 1) * QT], in0=dd[:, hf_, :],
                     scalar=gsc[:, hf_:hf_ + 1], in1=rstd[:, 0:QT], op0=ALU.mult, op1=ALU.mult,
                     reads=[hc, hl], writes=[hy])
        return finish

    for qt in range(NQT):
        for m in range(2):
            ps = dict(S=S, hS=hS, O=Om[m], hO=hOm[m], L=Lm[m], hL=hLm[m])
            attn_qt(s, cfg, QT, qt, KT[m], hK[m], QTs[m], hQ[m], V, hV, 2, None, mk, hm, ones_b, hones, ps, pts, hpts,
                    make_finish(m))
    for hf_ in range(2):
        s.dma("sp", d[10 + hf_], io["ymT"][256 + 128 * hf_:384 + 128 * hf_, :], yo[:, hf_, :], reads=[hy],
              writes=[io["hym"]])


def attn_qt(s, cfg, QT, qt, KT, hK, QTs, hQ, V, hV, ndv, bias_fn, mk, hm, ones_b, hones, ps, pts, hpts, finish):
    QB = QT // 128
    nkb = (qt + 1) * QB
    O, hO, L, hL = ps["O"], ps["hO"], ps["L"], ps["hL"]
    prev = None

    def pv(kb, pt, hpt):
        for v in range(ndv):
            s.op("pe", "matmul", O[v][:, 0:QT], lhsT=V[:, kb, v * 128:(v + 1) * 128], rhs=pt[:, 0:QT],
                 start=(kb == 0), stop=(kb == nkb - 1), reads=[hV, hpt], writes=[hO[v]])
        s.op("pe", "matmul", L[:, 0:QT], lhsT=ones_b[:], rhs=pt[:, 0:QT],
             start=(kb == 0), stop=(kb == nkb - 1), reads=[hones, hpt], writes=[hL])
    for kb in range(nkb):
        S, hS = ps["S"][kb % 2], ps["hS"][kb % 2]
        s.op("pe", "matmul", S[:, 0:QT], lhsT=KT[:, kb * 128:(kb + 1) * 128], rhs=QTs[:, qt * QT:(qt + 1) * QT],
             start=True, stop=True, reads=[hK, hQ], writes=[hS])
        i = PTC[0] % len(pts)
        PTC[0] += 1
        pt, hpt = pts[i], hpts[i]
        bias = bias_fn(kb, qt) if bias_fn is not None else 0.0
        rd = [hS] + ([ps["hbias"]] if bias_fn is not None else [])
        s.op("act", "activation", out=pt[:, 0:QT], in_=S[:, 0:QT], func=AF.Exp, scale=SCALE, bias=bias,
             reads=rd, writes=[hpt])
        j = kb - qt * QB
        if j >= 0:
            s.op("dve", "tensor_tensor", out=pt[:, 0:QT], in0=pt[:, 0:QT], in1=mk[:, j, 0:QT], op=ALU.mult,
                 reads=[hpt, hm], writes=[hpt])
        if prev is not None:
            pv(*prev)
        prev = (kb, pt, hpt)
    pv(*prev)
    finish(qt)


PTC = [0]


SCALE = 128 ** -0.5


def phase_DN_conv(s, cfg, io, stack, ones_f, honesf):
    T = cfg["T"]
    TS = min(2048, T)
    cw = s.sbuf("dncw", [128, 6, 4], F32, stack)
    hcw = H()
    dc = s.dsem("dncd")
    for j in range(4):
        s.dma("sp", dc, cw[:, :, j], io["dn_conv"][j].rearrange("(c p) -> p c", p=128), writes=[hcw],
              allow_slow_non_contiguous=True)
    NB = 2
    bufs = []
    for i in range(NB):
        bufs.append(dict(x=s.sbuf("dnx", [128, TS + 3], F32, stack), t=s.sbuf("dnt", [128, TS], F32, stack),
                         sq=s.sbuf("dnsq", [128, TS], F32, stack), rn=s.sbuf("dnrn", [128, TS], F32, stack),
                         h=[H() for _ in range(4)], d=[s.dsem("dnd") for _ in range(2)]))
    pss = [s.psum("dnss", [128, 512], F32, stack) for _ in range(2)]
    hpss = [H(), H()]
    it = 0
    pc = 0
    for i in range(6):
        eng = "dve"
        for t0 in range(0, T, TS):
            B = bufs[it % NB]
            it += 1
            x, t, sq, rn = B["x"], B["t"], B["sq"], B["rn"]
            hx, ht, hsq, hrn = B["h"]
            r = (6 + i) * 128
            if t0 == 0:
                s.op("pool", "memset", x[:, 0:3], 0.0, writes=[hx])
                s.dma("sp", B["d"][0], x[:, 3:], io["scr32"][r:r + 128, 0:TS], reads=[io["hscr"]], writes=[hx])
            else:
                s.dma("sp", B["d"][0], x[:], io["scr32"][r:r + 128, t0 - 3:t0 + TS], reads=[io["hscr"]], writes=[hx])
            s.op(eng, "tensor_scalar", out=t[:], in0=x[:, 3:TS + 3], scalar1=cw[:, i, 3:4], scalar2=None, op0=ALU.mult,
                 reads=[hx, hcw], writes=[ht])
            for j in (2, 1, 0):
                s.op(eng, "scalar_tensor_tensor", out=t[:], in0=x[:, j:TS + j], scalar=cw[:, i, j:j + 1], in1=t[:],
                     op0=ALU.mult, op1=ALU.add, reads=[hx, hcw, ht], writes=[ht])
            s.op("act", "activation", out=t[:], in_=t[:], func=AF.Silu, reads=[ht], writes=[ht])
            if i < 4:
                s.op("act", "activation", out=sq[:], in_=t[:], func=AF.Square, reads=[ht], writes=[hsq])
                for c0 in range(0, TS, 512):
                    w = min(512, TS - c0)
                    p, hp = pss[pc % 2], hpss[pc % 2]
                    pc += 1
                    s.op("pe", "matmul", p[:, 0:w], lhsT=ones_f[:], rhs=sq[:, c0:c0 + w], start=True, stop=True,
                         reads=[honesf, hsq], writes=[hp])
                    s.op("act", "activation", out=rn[:, c0:c0 + w], in_=p[:, 0:w], func=AF.Sqrt, bias=EPS, scale=1.0,
                         reads=[hp], writes=[hrn])
                s.op("dve", "reciprocal", out=rn[:], in_=rn[:], reads=[hrn], writes=[hrn])
                if i < 2:
                    s.op(eng, "scalar_tensor_tensor", out=t[:], in0=t[:], scalar=SCALE, in1=rn[:], op0=ALU.mult,
                         op1=ALU.mult, reads=[ht, hrn], writes=[ht])
                else:
                    s.op(eng, "tensor_tensor", out=t[:], in0=t[:], in1=rn[:], op=ALU.mult, reads=[ht, hrn], writes=[ht])
            s.dma("sp", B["d"][1], io["dnscr"][i * 128:(i + 1) * 128, t0:t0 + TS], t[:], reads=[ht],
                  writes=[io["hdnscr"]])


def phase_DN(s, cfg, io, stack, ones_f, honesf, tri, htri, ident_f, hidf):
    T = cfg["T"]
    C = 128
    NCH = T // C
    SC = min(8, NCH)
    NSC = NCH // SC
    triS = s.sbuf("triS", [128, 128], F32, stack)
    mlS = s.sbuf("mlS", [128, 128], F32, stack)
    hmk = H()
    s.op("pool", "tensor_tensor", out=triS[:], in0=tri[:], in1=ident_f[:], op=ALU.subtract, reads=[htri, hidf], writes=[hmk])
    s.op("pool", "memset", mlS[:], 1.0, writes=[hmk])
    s.op("pool", "affine_select", out=mlS[:], in_=mlS[:], pattern=[[-1, 128]], compare_op=ALU.is_gt, fill=0.0,
         base=0, channel_multiplier=1, reads=[hmk], writes=[hmk])
    gd = s.sbuf("dngd", [128, 1], F32, stack)
    hgd = H()
    dgl = s.dsem("dngl")
    s.dma("sp", dgl, gd[:], io["dn_norm"].rearrange("(p o) -> p o", o=1), writes=[hgd])
    PB = []
    for h in range(2):
        PB.append(dict(bp=[s.psum("dnbp", [128, 512], F32, stack) for _ in range(3)],
                       hbp=[HX() for _ in range(3)],
                       bs=s.psum("dnbs", [128, 512], F32, stack), hbs=HX()))
    pg = PB[0]["bs"]
    hpg = PB[0]["hbs"]
    HD = []
    for h in range(2):
        d = {}
        d.update(PB[h])
        acol = s.sbuf("dna", [128, NCH], F32, stack)
        bcol = s.sbuf("dnb", [128, NCH], F32, stack)
        sc2 = s.sbuf("dnsc", [128, 2], F32, stack)
        gneg = s.sbuf("dngn", [128, NCH], F32, stack)
        pcol = s.sbuf("dnpc", [128, NCH], F32, stack)
        nb = s.sbuf("dnnb", [128, NCH], F32, stack)
        sbg = s.sbuf("dnsbg", [128, NCH], F32, stack)
        sdec = s.sbuf("dnsdec", [128, NCH], F32, stack)
        sdl = s.sbuf("dnsdl", [128, NCH], F32, stack)
        hh = H()
        dd = [s.dsem("dnh") for _ in range(4)]
        r = N32 * 128
        s.dma("sp", dd[0], acol[:], io["scr32"][r + h, :].rearrange("(c p) -> p c", p=128), reads=[io["hscr"]],
              writes=[hh], allow_slow_non_contiguous=True)
        s.dma("sp", dd[1], bcol[:], io["scr32"][r + 2 + h, :].rearrange("(c p) -> p c", p=128), reads=[io["hscr"]],
              writes=[hh], allow_slow_non_contiguous=True)
        s.dma("sp", dd[2], sc2[:, 0:1], io["dn_a_log"][h:h + 1].partition_broadcast(128), writes=[hh])
        s.dma("sp", dd[3], sc2[:, 1:2], io["dn_dt_bias"][h:h + 1].partition_broadcast(128), writes=[hh])
        s.op("act", "activation", out=sc2[:, 0:1], in_=sc2[:, 0:1], func=AF.Exp, reads=[hh], writes=[hh])
        s.op("act", "activation", out=gneg[:], in_=acol[:], func=AF.Exp, scale=1.0, bias=sc2[:, 1:2], reads=[hh], writes=[hh])
        s.op("act", "activation", out=gneg[:], in_=gneg[:], func=AF.Ln, scale=1.0, bias=1.0, reads=[hh], writes=[hh])
        s.op("dve", "tensor_scalar", out=gneg[:], in0=gneg[:], scalar1=sc2[:, 0:1], scalar2=None, op0=ALU.mult,
             reads=[hh], writes=[hh])
        s.op("act", "activation", out=bcol[:], in_=bcol[:], func=AF.Sigmoid, reads=[hh], writes=[hh])
        s.op("dve", "tensor_scalar", out=nb[:], in0=bcol[:], scalar1=-1.0, scalar2=None, op0=ALU.mult, reads=[hh], writes=[hh])
        s.op("pe", "matmul", pg[:, 0:NCH], lhsT=tri[:], rhs=gneg[:], start=True, stop=True, reads=[htri, hh], writes=[hpg])
        s.op("pe", "matmul", pg[:, NCH:2 * NCH], lhsT=ones_f[:], rhs=gneg[:], start=True, stop=True,
             reads=[honesf, hh], writes=[hpg])
        s.op("dve", "tensor_copy", out=pcol[:], in_=pg[:, 0:NCH], reads=[hpg], writes=[hh])
        s.op("act", "activation", out=sbg[:], in_=pg[:, 0:NCH], func=AF.Exp, scale=-1.0, reads=[hpg], writes=[hh])
        s.op("dve", "tensor_tensor", out=sbg[:], in0=sbg[:], in1=bcol[:], op=ALU.mult, reads=[hh], writes=[hh])
        s.op("dve", "tensor_tensor", out=sdec[:], in0=pcol[:], in1=pg[:, NCH:2 * NCH], op=ALU.subtract, reads=[hpg, hh], writes=[hh])
        s.op("act", "activation", out=sdec[:], in_=sdec[:], func=AF.Exp, reads=[hh], writes=[hh])
        s.op("act", "activation", out=sdl[:], in_=pg[:, NCH:2 * NCH], func=AF.Exp, scale=-1.0, reads=[hpg], writes=[hh])
        d.update(gneg=gneg, pcol=pcol, bcol=bcol, nb=nb, sbg=sbg, sdec=sdec, sdl=sdl, hh=hh)
        d["S"] = s.sbuf("dnS", [128, 128], F32, stack)
        d["hS"] = H()
        s.op("pool", "memset", d["S"][:], 0.0, writes=[d["hS"]])
        d["yo"] = s.sbuf("dnyo", [128, T], BF16, stack)
        d["hyo"] = H()
        d["slots"] = []
        for k in range(2):
            sl = {n: s.sbuf("dn" + n, [128, 128], F32, stack) for n in ("wT", "u", "qgT", "aqkT", "kdec")}
            sl["h"] = {n: H() for n in ("wT", "u", "qgT", "aqkT", "kdec")}
            d["slots"].append(sl)
        d["w"] = {n: s.sbuf("dnw" + n, [128, 128], F32, stack) for n in
                  ("xp", "xn", "eupi", "eups", "elos", "tmp", "M", "Z", "kbg", "vb", "erow", "nbrow", "rhs2")}
        d["w"]["NM"] = [s.sbuf("dnwNM", [128, 256], F32, stack) for _ in range(2)]
        d["w"]["rhs2"] = s.sbuf("dnwrhs2", [128, 256], F32, stack)
        d["hw"] = {n: H() for n in list(d["w"].keys()) + ["NM0", "NM1"]}
        d["sw"] = {n: s.sbuf("dns" + n, [128, 128], F32, stack) for n in ("vnew", "on", "junk")}
        d["sw"]["rs"] = s.sbuf("dnsrs", [128, 2], F32, stack)
        d["hsw"] = {n: H() for n in ("vnew", "on", "junk", "rs")}
        HD.append(d)
    SW = SC * C
    tiles = []
    for k in range(2):
        tl = dict(q=[s.sbuf("dnq", [128, SW], F32, stack) for _ in range(2)],
                  k=[s.sbuf("dnk", [128, SW], F32, stack) for _ in range(2)],
                  v=[s.sbuf("dnv", [128, SW], F32, stack) for _ in range(2)],
                  z=[s.sbuf("dnz", [128, SW], F32, stack) for _ in range(2)],
                  h=[H() for _ in range(8)], d=[s.dsem("dnt") for _ in range(8)])
        tiles.append(tl)

    def load_sc(sci):
        tl = tiles[sci % 2]
        c0 = sci * SW
        for h in range(2):
            for j, nm in enumerate(("q", "k", "v")):
                blk = 2 * j + h
                s.dma("sp", tl["d"][blk], tl[nm][h][:], io["dnscr"][blk * 128:(blk + 1) * 128, c0:c0 + SW],
                      reads=[io["hdnscr"]], writes=[tl["h"][blk]])
            r = (12 + h) * 128
            s.dma("sp", tl["d"][6 + h], tl["z"][h][:], io["scr32"][r:r + 128, c0:c0 + SW], reads=[io["hscr"]],
                  writes=[tl["h"][6 + h]])
            s.op("act", "activation", out=tl["z"][h][:], in_=tl["z"][h][:], func=AF.Silu, reads=[tl["h"][6 + h]],
                 writes=[tl["h"][6 + h]])

    def prep(c, h):
        d = HD[h]
        tl = tiles[(c // SC) % 2]
        cl = (c % SC) * C
        qT, kT, vT = tl["q"][h][:, cl:cl + C], tl["k"][h][:, cl:cl + C], tl["v"][h][:, cl:cl + C]
        hq, hk, hv = tl["h"][h], tl["h"][2 + h], tl["h"][4 + h]
        w, hw, bp, hbp = d["w"], d["hw"], d["bp"], d["hbp"]
        sl = d["slots"][c % 2]
        hsl = sl["h"]
        hh = d["hh"]
        s.op("pool", "tensor_scalar", out=w["rhs2"][:, 0:128], in0=tri[:], scalar1=d["gneg"][:, c:c + 1], scalar2=None,
             op0=ALU.mult, reads=[htri, hh], writes=[hw["rhs2"]])
        s.op("pool", "tensor_scalar", out=w["rhs2"][:, 128:256], in0=ident_f[:], scalar1=d["nb"][:, c:c + 1], scalar2=None,
             op0=ALU.mult, reads=[hidf, hh], writes=[hw["rhs2"]])
        s.op("pe", "matmul", bp[0][:, 0:128], lhsT=kT, rhs=kT, start=True, stop=True, reads=[hk], writes=[hbp[0]])
        s.op("pe", "matmul", bp[0][:, 128:256], lhsT=kT, rhs=qT, start=True, stop=True, reads=[hk, hq], writes=[hbp[0]])
        s.op("pe", "matmul", bp[0][:, 256:512], lhsT=ones_f[:], rhs=w["rhs2"][:], start=True, stop=True,
             reads=[honesf, hw["rhs2"]], writes=[hbp[0]])
        s.op("pe", "transpose", bp[2][:, 0:128], kT, ident_f[:], reads=[hk, hidf], writes=[hbp[2]])
        s.op("pe", "transpose", bp[2][:, 128:256], vT, ident_f[:], reads=[hv, hidf], writes=[hbp[2]])
        s.op("dve", "tensor_scalar", out=w["xp"][:], in0=bp[0][:, 256:384], scalar1=d["pcol"][:, c:c + 1], scalar2=0.0,
             op0=ALU.subtract, op1=ALU.max, reads=[hbp[0], hh], writes=[hw["xp"]])
        s.op("dve", "tensor_scalar", out=w["xn"][:], in0=bp[0][:, 256:384], scalar1=d["pcol"][:, c:c + 1], scalar2=0.0,
             op0=ALU.subtract, op1=ALU.min, reads=[hbp[0], hh], writes=[hw["xn"]])
        s.op("act", "activation", out=w["xp"][:], in_=w["xp"][:], func=AF.Exp, scale=-1.0, reads=[hw["xp"]], writes=[hw["xp"]])
        s.op("act", "activation", out=w["xn"][:], in_=w["xn"][:], func=AF.Exp, reads=[hw["xn"]], writes=[hw["xn"]])
        s.op("act", "activation", out=w["erow"][:], in_=bp[0][:, 256:384], func=AF.Exp, scale=-1.0, reads=[hbp[0]],
             writes=[hw["erow"]])
        s.op("act", "activation", out=w["nbrow"][:], in_=bp[0][:, 384:512], func=AF.Copy, reads=[hbp[0]], writes=[hw["nbrow"]])
        s.op("pool", "tensor_tensor", out=w["eupi"][:], in0=w["xp"][:], in1=tri[:], op=ALU.mult, reads=[hw["xp"], htri], writes=[hw["eupi"]])
        s.op("pool", "tensor_tensor", out=w["eups"][:], in0=w["xp"][:], in1=triS[:], op=ALU.mult, reads=[hw["xp"], hmk], writes=[hw["eups"]])
        s.op("pool", "tensor_tensor", out=w["eups"][:], in0=w["eups"][:], in1=w["nbrow"][:], op=ALU.mult,
             reads=[hw["eups"], hw["nbrow"]], writes=[hw["eups"]])
        s.op("pool", "tensor_tensor", out=w["elos"][:], in0=w["xn"][:], in1=mlS[:], op=ALU.mult, reads=[hw["xn"], hmk], writes=[hw["elos"]])
        s.op("dve", "tensor_tensor", out=sl["aqkT"][:], in0=bp[0][:, 128:256], in1=w["eupi"][:], op=ALU.mult,
             reads=[hbp[0], hw["eupi"]], writes=[hsl["aqkT"]])
        NM = w["NM"]
        s.op("dve", "tensor_tensor", out=NM[0][:, 128:256], in0=bp[0][:, 0:128], in1=w["eups"][:], op=ALU.mult,
             reads=[hbp[0], hw["eups"]], writes=[hw["NM0"]])
        s.op("dve", "scalar_tensor_tensor", out=NM[0][:, 0:128], in0=bp[0][:, 0:128], scalar=d["nb"][:, c:c + 1],
             in1=w["elos"][:], op0=ALU.mult, op1=ALU.mult, reads=[hbp[0], hw["elos"], hh], writes=[hw["NM0"]])
        s.op("pool", "tensor_tensor", out=w["Z"][:], in0=NM[0][:, 128:256], in1=ident_f[:], op=ALU.add,
             reads=[hw["NM0"], hidf], writes=[hw["Z"]])
        s.op("pool", "tensor_tensor", out=sl["qgT"][:], in0=qT, in1=w["erow"][:], op=ALU.mult, reads=[hq, hw["erow"]],
             writes=[hsl["qgT"]])
        s.op("dve", "tensor_scalar", out=w["kbg"][:], in0=bp[2][:, 0:128], scalar1=d["sbg"][:, c:c + 1], scalar2=None,
             op0=ALU.mult, reads=[hbp[2], hh], writes=[hw["kbg"]])
        s.op("act", "activation", out=sl["kdec"][:], in_=bp[2][:, 0:128], func=AF.Copy, scale=d["sdec"][:, c:c + 1],
             reads=[hbp[2], hh], writes=[hsl["kdec"]])
        s.op("act", "activation", out=w["vb"][:], in_=bp[2][:, 128:256], func=AF.Copy, scale=d["bcol"][:, c:c + 1],
             reads=[hbp[2], hh], writes=[hw["vb"]])
        cur = 0
        for lv in range(1, 7):
            Ncur, Mcur = NM[cur][:, 0:128], NM[cur][:, 128:256]
            hcur = hw["NM%d" % cur]
            nxt = 1 - cur
            hn = hw["NM%d" % nxt]
            s.op("pe", "matmul", bp[1][:, 0:128], lhsT=Mcur, rhs=Ncur, start=True, stop=True, reads=[hcur], writes=[hbp[1]])
            if lv < 6:
                s.op("pe", "matmul", bp[1][:, 128:256], lhsT=Ncur, rhs=Mcur, start=True, stop=True, reads=[hcur],
                     writes=[hbp[1]])
                s.op("act", "activation", out=NM[nxt][:], in_=bp[1][:, 0:256], func=AF.Copy, reads=[hbp[1]], writes=[hn])
            else:
                s.op("act", "activation", out=NM[nxt][:, 0:128], in_=bp[1][:, 0:128], func=AF.Copy, reads=[hbp[1]],
                     writes=[hn])
            s.op("pe", "matmul", bp[1][:, 256:384], lhsT=NM[nxt][:, 0:128], rhs=w["Z"][:], start=True, stop=True,
                 reads=[hn, hw["Z"]], writes=[hbp[1]])
            s.op("dve", "tensor_tensor", out=w["Z"][:], in0=bp[1][:, 256:384], in1=w["Z"][:], op=ALU.add,
                 reads=[hbp[1], hw["Z"]], writes=[hw["Z"]])
            cur = nxt
        s.op("pe", "matmul", bp[2][:, 256:384], lhsT=w["kbg"][:], rhs=w["Z"][:], start=True, stop=True,
             reads=[hw["kbg"], hw["Z"]], writes=[hbp[2]])
        s.op("pe", "matmul", bp[2][:, 384:512], lhsT=w["Z"][:], rhs=w["vb"][:], start=True, stop=True,
             reads=[hw["vb"], hw["Z"]], writes=[hbp[2]])
        s.op("act", "activation", out=sl["wT"][:], in_=bp[2][:, 256:384], func=AF.Copy, reads=[hbp[2]], writes=[hsl["wT"]])
        s.op("act", "activation", out=sl["u"][:], in_=bp[2][:, 384:512], func=AF.Copy, reads=[hbp[2]], writes=[hsl["u"]])

    def scan(c, h):
        d = HD[h]
        tl = tiles[(c // SC) % 2]
        cl = (c % SC) * C
        sl = d["slots"][c % 2]
        hsl = sl["h"]
        S, hS = d["S"], d["hS"]
        bs, hbs = d["bs"], d["hbs"]
        sw, hsw = d["sw"], d["hsw"]
        hh = d["hh"]
        s.op("pe", "matmul", bs[:, 0:128], lhsT=sl["wT"][:], rhs=S[:], start=True, stop=True, reads=[hsl["wT"], hS],
             writes=[hbs])
        s.op("dve", "tensor_tensor", out=sw["vnew"][:], in0=sl["u"][:], in1=bs[:, 0:128], op=ALU.subtract,
             reads=[hsl["u"], hbs], writes=[hsw["vnew"]])
        s.op("pe", "matmul", bs[:, 128:256], lhsT=sl["qgT"][:], rhs=S[:], start=True, stop=False, reads=[hsl["qgT"], hS],
             writes=[hbs])
        s.op("pe", "matmul", bs[:, 128:256], lhsT=sl["aqkT"][:], rhs=sw["vnew"][:], start=False, stop=True,
             reads=[hsl["aqkT"], hsw["vnew"]], writes=[hbs])
        s.op("pe", "matmul", bs[:, 256:384], lhsT=sl["kdec"][:], rhs=sw["vnew"][:], start=True, stop=True,
             reads=[hsl["kdec"], hsw["vnew"]], writes=[hbs])
        s.op("dve", "scalar_tensor_tensor", out=S[:], in0=S[:], scalar=d["sdl"][:, c:c + 1], in1=bs[:, 256:384],
             op0=ALU.mult, op1=ALU.add, reads=[hS, hbs, hh], writes=[hS])
        rs = sw["rs"]
        s.op("act", "activation", out=sw["junk"][:], in_=bs[:, 128:256], func=AF.Square, accum_out=rs[:, 0:1],
             reads=[hbs], writes=[hsw["junk"], hsw["rs"]])
        s.op("act", "activation", out=rs[:, 1:2], in_=rs[:, 0:1], func=AF.Sqrt, scale=1.0 / 128, bias=EPS,
             reads=[hsw["rs"]], writes=[hsw["rs"]])
        s.op("dve", "reciprocal", out=rs[:, 1:2], in_=rs[:, 1:2], reads=[hsw["rs"]], writes=[hsw["rs"]])
        s.op("act", "activation", out=sw["on"][:], in_=bs[:, 128:256], func=AF.Copy, scale=rs[:, 1:2],
             reads=[hbs, hsw["rs"]], writes=[hsw["on"]])
        s.op("pe", "transpose", bs[:, 384:512], sw["on"][:], ident_f[:], reads=[hsw["on"], hidf], writes=[hbs])
        s.op("dve", "scalar_tensor_tensor", out=d["yo"][:, c * C:(c + 1) * C], in0=bs[:, 384:512], scalar=gd[:, 0:1],
             in1=tl["z"][h][:, cl:cl + C], op0=ALU.mult, op1=ALU.mult, reads=[hbs, hgd, tl["h"][6 + h]],
             writes=[d["hyo"]])

    import os
    STAGE = int(os.environ.get("DN_STAGE", "3"))
    if STAGE == 0:
        return
    load_sc(0)
    if STAGE == 1:
        return
    for h in range(2):
        prep(0, h)
    if STAGE == 2:
        return
    LIM = int(os.environ.get("DN_LIM", "100000"))
    NOPOST = int(os.environ.get("DN_NOPOST", "0"))
    for c in range(min(NCH, LIM)):
        if c % SC == 0 and c // SC + 1 < NSC:
            load_sc(c // SC + 1)
        if c + 1 < min(NCH, LIM):
            for h in range(2):
                prep(c + 1, h)
        for h in range(2):
            scan(c, h)
    dout = [s.dsem("dnout") for _ in range(2)]
    for h in range(2):
        s.dma("sp", dout[h], io["ymT"][512 + 128 * h:640 + 128 * h, :], HD[h]["yo"][:], reads=[HD[h]["hyo"]],
              writes=[io["hym"]])


def f_tiles(cfg):
    NT, TT = cfg["NT"], cfg["TT"]
    tiles = [(0, 2, True)]
    for i in range(NT // TT):
        tiles.append((2 + i * TT, TT, False))
    return tiles


def dff_parts(cfg):
    DFF, PB = cfg["DFF"], cfg["PB"]
    tiles = []
    c = 0
    while c < DFF:
        w = min(512, DFF - c)
        tiles.append((c, w))
        c += w
    parts = []
    cur = []
    nb = 0
    for (c0, w) in tiles:
        if nb + w // 128 > PB:
            parts.append(cur)
            cur = []
            nb = 0
        cur.append((c0, w))
        nb += w // 128
    if cur:
        parts.append(cur)
    return parts


def phase_F(s, cfg, io, ws, last, plan_only, st=None):
    D, DFF, DM, NT, TT, PB = cfg["D"], cfg["DFF"], cfg["DM"], cfg["NT"], cfg["TT"], cfg["PB"]
    KC = D // 128
    KCM = DM // 128
    NTB = TT // 128
    parts = dff_parts(cfg)
    tiles = f_tiles(cfg)
    NDT = D // 512 if D >= 512 else 1
    DW = min(512, D)

    if not plan_only:
        xs = st["xs"]; hT = st["hT"]; act = st["act"]; hn = st["hn"]
        hxs, hhT, hact, hhn = st["hxs"], st["hhT"], st["hact"], st["hhn"]
        ps, hps = st["ps"], st["hps"]
        pst = [ps[6][:].bitcast(BF16).rearrange("p (a b) -> p a b", b=128),
               ps[7][:].bitcast(BF16).rearrange("p (a b) -> p a b", b=128)]
        hpst = [hps[6], hps[7]]
        G, hG, tt, htt = st["G"], st["hG"], st["t"], st["ht"]
        carry, hcarry = st["carry"], st["hcarry"]
        cw, hcw = st["cw"], st["hcw"]
        gcol, hgcol = st["gcol"], st["hgcol"]
        ident, hid = st["ident"], st["hid"]
        rs = st["rs"]
        dl, dst_ = st["dl"], st["dst"]
        hout = st["hout"]

    for (col0, ntok, is_halo) in tiles:
        tbs = [(o, min(128, ntok - o)) for o in range(0, ntok, 128)]
        if not plan_only:
            for tb, (o, n) in enumerate(tbs):
                s.dma("sp", st["dlx"][tb], xs[0:n, tb, :], io["xh"][col0 + o:col0 + o + n, :], writes=[hxs[tb]])
            s.dma("sp", st["dla"], act[:, 0:KCM, 0:ntok],
                  io["ymT"][:, col0:col0 + ntok].rearrange("(c p) n -> p c n", p=128), writes=[hact])
        for nt in range(NDT):
            pb = (nt % 2) * 4
            for kp in range(0, KCM, 8):
                nk = min(8, KCM - kp)
                if plan_only:
                    ws.add(io["w_out"], kp, nk, nt * DW, DW, "wo")
                    continue
                slot, hs, nkc, width = ws.get("wo")
                for kcl in range(nkc):
                    kc = kp + kcl
                    for tb, (o, n) in enumerate(tbs):
                        s.op("pe", "matmul", ps[pb + tb][0:n, 0:DW], lhsT=act[:, kc, o:o + n], rhs=slot[:, kcl, 0:DW],
                             start=(kc == 0), stop=(kc == KCM - 1), reads=[hact, hs], writes=[hps[pb + tb]])
                ws.done()
            if not plan_only:
                for tb, (o, n) in enumerate(tbs):
                    s.op("dve", "tensor_tensor", out=xs[0:n, tb, nt * DW:(nt + 1) * DW], in0=ps[pb + tb][0:n, 0:DW],
                         in1=xs[0:n, tb, nt * DW:(nt + 1) * DW], op=ALU.add,
                         reads=[hps[pb + tb], hxs[tb]], writes=[hxs[tb]])
        if not plan_only:
            for tb, (o, n) in enumerate(tbs):
                norm_transpose(s, cfg, xs[0:n, tb, :], hxs[tb], n, hn, hhn, rs, ident, hid, pst, hpst,
                               hT, hhT, o, gcol, hgcol)
        if not plan_only and not is_halo and "dbg_x1" in io:
            for tb, (o, n) in enumerate(tbs):
                r0 = col0 - 2 + o
                s.dma("sp", dst_, io["dbg_x1"][r0:r0 + n, :], xs[0:n, tb, :], reads=[hxs[tb]], writes=[hout])
            s.dma("sp", dst_, io["dbg_hT"][:, col0 - 2:col0 - 2 + ntok].rearrange("(c p) n -> p c n", p=128),
                  hT[:, :, 0:ntok], reads=[hhT], writes=[hout])
        for pi, part in enumerate(parts):
            blk_l = 0
            for (c0, w) in part:
                nb = w // 128
                for kp in range(0, KC, 8):
                    nk = min(8, KC - kp)
                    if plan_only:
                        ws.add(io["w_gate"], kp, nk, c0, w, "wg")
                        continue
                    slot, hs, nkc, width = ws.get("wg")
                    for kcl in range(nkc):
                        kc = kp + kcl
                        for b in range(nb):
                            s.op("pe", "matmul", ps[b][:, 0:ntok], lhsT=slot[:, kcl, b * 128:(b + 1) * 128],
                                 rhs=hT[:, kc, 0:ntok], start=(kc == 0), stop=(kc == KC - 1),
                                 reads=[hhT, hs], writes=[hps[b]])
                    ws.done()
                if not plan_only:
                    for b in range(nb):
                        blk = c0 // 128 + b
                        if is_halo:
                            s.op("act", "activation", out=carry[:, blk, :], in_=ps[b][:, 0:2], func=AF.Copy,
                                 reads=[hps[b]], writes=[hcarry])
                            continue
                        s.op("act", "activation", out=G[b][:, 2:2 + ntok], in_=ps[b][:, 0:ntok], func=AF.Copy,
                             reads=[hps[b]], writes=[hG[b]])
                        s.op("dve", "tensor_copy", out=G[b][:, 0:2], in_=carry[:, blk, :], reads=[hcarry], writes=[hG[b]])
                        s.op("dve", "tensor_scalar", out=tt[b][:, 0:ntok], in0=G[b][:, 2:2 + ntok],
                             scalar1=cw[:, blk, 2:3], scalar2=None, op0=ALU.mult, reads=[hG[b], hcw], writes=[htt[b]])
                        s.op("dve", "scalar_tensor_tensor", out=tt[b][:, 0:ntok], in0=G[b][:, 1:1 + ntok],
                             scalar=cw[:, blk, 1:2], in1=tt[b][:, 0:ntok], op0=ALU.mult, op1=ALU.add,
                             reads=[hG[b], hcw, htt[b]], writes=[htt[b]])
                        s.op("dve", "scalar_tensor_tensor", out=tt[b][:, 0:ntok], in0=G[b][:, 0:ntok],
                             scalar=cw[:, blk, 0:1], in1=tt[b][:, 0:ntok], op0=ALU.mult, op1=ALU.add,
                             reads=[hG[b], hcw, htt[b]], writes=[htt[b]])
                        s.op("dve", "tensor_copy", out=carry[:, blk, :], in_=G[b][:, ntok:ntok + 2],
                             reads=[hG[b]], writes=[hcarry])
                        s.op("act", "activation", out=tt[b][:, 0:ntok], in_=tt[b][:, 0:ntok], func=AF.Silu,
                             reads=[htt[b]], writes=[htt[b]])
                if is_halo:
                    continue
                for kp in range(0, KC, 8):
                    nk = min(8, KC - kp)
                    if plan_only:
                        ws.add(io["w_up"], kp, nk, c0, w, "wu")
                        continue
                    slot, hs, nkc, width = ws.get("wu")
                    for kcl in range(nkc):
                        kc = kp + kcl
                        for b in range(nb):
                            s.op("pe", "matmul", ps[4 + b][:, 0:ntok],
                                 lhsT=slot[:, kcl, b * 128:(b + 1) * 128],
                                 rhs=hT[:, kc, 0:ntok], start=(kc == 0), stop=(kc == KC - 1),
                                 reads=[hhT, hs], writes=[hps[4 + b]])
                    ws.done()
                if not plan_only:
                    for b in range(nb):
                        s.op("dve", "tensor_tensor", out=act[:, blk_l + b, 0:ntok], in0=ps[4 + b][:, 0:ntok],
                             in1=tt[b][:, 0:ntok], op=ALU.mult, reads=[hps[4 + b], htt[b]], writes=[hact])
                blk_l += nb
            if is_halo:
                continue
            part_c0 = part[0][0]
            part_nb = blk_l
            kc_first = (pi == 0)
            for nt in range(NDT):
                pb = (nt % 2) * 4
                for kp in range(0, part_nb, 8):
                    nk = min(8, part_nb - kp)
                    if plan_only:
                        ws.add(io["w_down"], part_c0 // 128 + kp, nk, nt * DW, DW, "wd")
                        continue
                    slot, hs, nkc, width = ws.get("wd")
                    for kcl in range(nkc):
                        kc = kp + kcl
                        for tb, (o, n) in enumerate(tbs):
                            s.op("pe", "matmul", ps[pb + tb][0:n, 0:DW], lhsT=act[:, kc, o:o + n], rhs=slot[:, kcl, 0:DW],
                                 start=(kc == 0), stop=(kc == part_nb - 1), reads=[hact, hs], writes=[hps[pb + tb]])
                    ws.done()
                if not plan_only:
                    for tb, (o, n) in enumerate(tbs):
                        s.op("dve", "tensor_tensor", out=xs[0:n, tb, nt * DW:(nt + 1) * DW], in0=ps[pb + tb][0:n, 0:DW],
                             in1=xs[0:n, tb, nt * DW:(nt + 1) * DW], op=ALU.add,
                             reads=[hps[pb + tb], hxs[tb]], writes=[hxs[tb]])
        if is_halo or plan_only:
            continue
        if last:
            gbt = st["gfin"]
            for tb, (o, n) in enumerate(tbs):
                ss, rstd, hrs = rs
                s.op("act", "activation", out=hn[0:n, :], in_=xs[0:n, tb, :], func=AF.Square, accum_out=ss[0:n, :],
                     reads=[hxs[tb]], writes=[hhn, hrs])
                s.op("act", "activation", out=rstd[0:n, :], in_=ss[0:n, :], func=AF.Sqrt, scale=1.0 / D, bias=EPS,
                     reads=[hrs], writes=[hrs])
                s.op("dve", "reciprocal", out=rstd[0:n, :], in_=rstd[0:n, :], reads=[hrs], writes=[hrs])
                s.op("dve", "scalar_tensor_tensor", out=xs[0:n, tb, :], in0=xs[0:n, tb, :], scalar=rstd[0:n, :],
                     in1=gbt[0:n, :], op0=ALU.mult, op1=ALU.mult, reads=[hxs[tb], hrs, st["hgfin"]], writes=[hxs[tb]])
        for tb, (o, n) in enumerate(tbs):
            r0 = col0 - 2 + o
            s.dma("sp", st["dsx"][tb], io["out"][r0:r0 + n, :], xs[0:n, tb, :], reads=[hxs[tb]], writes=[hout])


def setup_F(s, cfg, io, last):
    D, DFF, DM, NT, TT, PB = cfg["D"], cfg["DFF"], cfg["DM"], cfg["NT"], cfg["TT"], cfg["PB"]
    KC = D // 128
    NTB = TT // 128
    st = {}
    st["xs"] = s.sbuf("xs", [128, NTB, D], F32)
    st["hxs"] = [H() for _ in range(NTB)]
    st["hT"] = s.sbuf("hT", [128, KC, TT], BF16)
    st["hhT"] = H()
    nact = max(PB, DM // 128)
    st["act"] = s.sbuf("act", [128, nact, TT], BF16)
    st["hact"] = H()
    st["hn"] = s.sbuf("hn", [128, D], BF16)
    st["hhn"] = H()
    st["ps"] = [s.psum("ps", [128, 512], F32) for _ in range(8)]
    st["hps"] = [H() for _ in range(8)]
    st["G"] = [s.sbuf("G", [128, TT + 2], F32) for _ in range(4)]
    st["hG"] = [H() for _ in range(4)]
    st["t"] = [s.sbuf("t", [128, TT], F32) for _ in range(4)]
    st["ht"] = [H() for _ in range(4)]
    NBLK = DFF // 128
    st["carry"] = s.sbuf("carry", [128, NBLK, 2], F32)
    st["hcarry"] = H()
    st["cw"] = s.sbuf("cw", [128, NBLK, 3], F32)
    st["hcw"] = H()
    st["gcol"] = s.sbuf("gcol", [128, KC], F32)
    st["hgcol"] = H()
    st["rs"] = (s.sbuf("ss", [128, 1], F32), s.sbuf("rstd", [128, 1], F32), H())
    st["dl"] = s.dsem("dl")
    st["dlx"] = [s.dsem("dlx") for _ in range(NTB)]
    st["dsx"] = [s.dsem("dsx") for _ in range(NTB)]
    st["dla"] = s.dsem("dla")
    st["dst"] = s.dsem("dst")
    st["hout"] = H()
    st["ident"], st["hid"] = make_identity(s, BF16)
    s.dma("sp", st["dl"], st["gcol"][:], io["ffn_norm"].rearrange("(c p) -> p c", p=128), writes=[st["hgcol"]],
          allow_slow_non_contiguous=True)
    for j in range(3):
        s.dma("sp", st["dl"], st["cw"][:, :, j], io["ffn_conv"][j].rearrange("(c p) -> p c", p=128),
              writes=[st["hcw"]], allow_slow_non_contiguous=True)
    if last:
        st["gfin"] = s.sbuf("gfin", [128, D], F32)
        st["hgfin"] = H()
        s.dma("sp", st["dl"], st["gfin"][:], io["final_norm"].partition_broadcast(128), writes=[st["hgfin"]])
    return st


import math
import ml_dtypes
from concourse.bass_utils import run_bass_kernel_spmd

T_SEQ = 8192
D_MODEL = 4096
D_FF = 11008
N_CORES = 8
GW = 1024
CFG_M = dict(D=D_MODEL, T=T_SEQ, TT1=512)
CFG_F = dict(D=D_MODEL, DFF=D_FF, DM=4096, NT=2048, TT=512, PB=32)


def make_consts(s):
    c = {}
    c["ident_b"], c["hid_b"] = make_identity(s, BF16)
    c["ident_f"], c["hidf"] = make_identity(s, F32)
    c["ones_b"] = s.sbuf("ones_b", [128, 128], BF16); c["hones"] = H()
    c["ones_f"] = s.sbuf("ones_f", [128, 128], F32); c["honesf"] = H()
    c["tri"] = s.sbuf("tri", [128, 128], F32); c["htri"] = H()
    s.op("pool", "memset", c["ones_b"][:], 1.0, writes=[c["hones"]])
    s.op("pool", "memset", c["ones_f"][:], 1.0, writes=[c["honesf"]])
    s.op("pool", "memset", c["tri"][:], 1.0, writes=[c["htri"]])
    s.op("pool", "affine_select", out=c["tri"][:], in_=c["tri"][:], pattern=[[1, 128]], compare_op=ALU.is_ge, fill=0.0,
         base=0, channel_multiplier=-1, reads=[c["htri"]], writes=[c["htri"]])
    return c


def m_io(nc, cfg, pfx=""):
    D, T = cfg["D"], cfg["T"]
    io = {}

    def inp(name, shape, dt=F32):
        io[name] = nc.dram_tensor(pfx + name, shape, dt, kind="ExternalInput").ap()
    inp("xb", [T, D]); inp("attn_norm", [D]); inp("wg", [D, WGW]); inp("sc_conv", [3, 256])
    inp("fox_bias", [2]); inp("lam_q1", [128]); inp("lam_k1", [128]); inp("lam_q2", [128]); inp("lam_k2", [128])
    inp("diff_norm", [256]); inp("dn_conv", [4, 768]); inp("dn_a_log", [2]); inp("dn_dt_bias", [2]); inp("dn_norm", [128])
    io["scr32"] = nc.dram_tensor(pfx + "scr32", [R32, T], F32).ap()
    io["scr16"] = nc.dram_tensor(pfx + "scr16", [R16, T], BF16).ap()
    io["vtm"] = nc.dram_tensor(pfx + "vtm", [T, TMW], BF16).ap()
    io["dnscr"] = nc.dram_tensor(pfx + "dnscr", [768, T], F32).ap()
    io["hscr"] = H(); io["hym"] = H(); io["hdnscr"] = H()
    return io


def run_M(s, cfg, io, ws, consts, lam_init):
    io["ident_b"], io["hid_b"] = consts["ident_b"], consts["hid_b"]
    ones_b, hones, ones_f, honesf = consts["ones_b"], consts["hones"], consts["ones_f"], consts["honesf"]
    tri, htri, ident_f, hidf = consts["tri"], consts["htri"], consts["ident_f"], consts["hidf"]
    with contextlib.ExitStack() as stack:
        st = setup_P1(s, cfg, io, stack)
        phase_P1(s, cfg, io, ws, False, st)
        s.barrier()
    with contextlib.ExitStack() as stack:
        mk, hm = make_masks(s, stack)
        with contextlib.ExitStack() as st2:
            phase_SC(s, cfg, io, st2); s.barrier()
        with contextlib.ExitStack() as st2:
            phase_FOX(s, cfg, io, st2, mk, hm, ones_b, hones, ones_f, honesf, tri, htri); s.barrier()
        with contextlib.ExitStack() as st2:
            phase_DIFF(s, cfg, io, st2, mk, hm, ones_b, hones, ones_f, honesf, lam_init); s.barrier()
        with contextlib.ExitStack() as st2:
            phase_DN_conv(s, cfg, io, st2, ones_f, honesf); s.barrier()
        with contextlib.ExitStack() as st2:
            phase_DN(s, cfg, io, st2, ones_f, honesf, tri, htri, ident_f, hidf); s.barrier()


def build_M(cfg, lam_init):
    nc = bass.Bass("TRN2", target_bir_lowering=False)
    io = m_io(nc, cfg)
    io["ymT"] = nc.dram_tensor("ymT", [1024, cfg["T"]], BF16, kind="ExternalOutput").ap()
    s = Sched(nc)
    consts = make_consts(s)
    ws = WStream(s, 6)
    phase_P1(s, cfg, io, ws, True)
    ws.start()
    run_M(s, cfg, io, ws, consts, lam_init)
    s.barrier()
    s.emit()
    s.close()
    return nc


def f_io(nc, cfg, last, pfx=""):
    D, DFF, DM, NT = cfg["D"], cfg["DFF"], cfg["DM"], cfg["NT"]
    io = {}

    def inp(name, shape, dt=F32):
        io[name] = nc.dram_tensor(pfx + name, shape, dt, kind="ExternalInput").ap()
    inp("xh", [NT + 2, D]); inp("ymT", [DM, NT + 2], BF16); inp("w_out", [DM, D]); inp("ffn_norm", [D])
    inp("w_gate", [D, DFF]); inp("w_up", [D, DFF]); inp("ffn_conv", [3, DFF]); inp("w_down", [DFF, D])
    if last:
        inp("final_norm", [D])
    return io


def build_F(cfg, last):
    nc = bass.Bass("TRN2", target_bir_lowering=False)
    io = f_io(nc, cfg, last)
    io["out"] = nc.dram_tensor("out", [cfg["NT"], cfg["D"]], F32, kind="ExternalOutput").ap()
    s = Sched(nc)
    ws = WStream(s, 4 if last else 6)
    phase_F(s, cfg, io, ws, last, True)
    st = setup_F(s, cfg, io, last)
    ws.start()
    phase_F(s, cfg, io, ws, last, False, st)
    s.barrier()
    s.emit()
    s.close()
    return nc


_OFF = {}
_o = 0
for _n, _w in (("sc_h", GW), ("sc_c", GW), ("sc_b", GW), ("df_q", GW), ("df_k", GW), ("df_v", GW),
               ("dn_q", GW), ("dn_k", GW), ("dn_v", GW), ("dn_z", GW), ("dn_a", 8), ("dn_b", 8),
               ("fx_q", GW), ("fx_k", GW), ("fx_v", GW), ("fx_f", 8)):
    _OFF[_n] = _o
    _o += _w


def wg_cols(g):
    r = lambda name, a, n: list(range(_OFF[name] + a, _OFF[name] + a + n))
    cols = []
    for nm in ("sc_h", "sc_c", "sc_b", "dn_q", "dn_k", "dn_v", "dn_z", "df_q", "df_k", "fx_q", "fx_k"):
        cols += r(nm, 256 * g, 256)
    cols += r("dn_a", 2 * g, 2) + r("dn_b", 2 * g, 2) + r("fx_f", 2 * g, 2)
    cols += r("df_v", 256 * g, 256) + r("fx_v", 256 * g, 256)
    assert len(cols) == WGW
    return np.array(cols)


def wout_rows():
    rows = []
    for g in range(4):
        for grp in range(4):
            rows += list(range(grp * GW + 256 * g, grp * GW + 256 * g + 256))
    return np.array(rows)


_PROG = {}


def _prog(key, fn):
    if key not in _PROG:
        _PROG[key] = fn()
    return _PROG[key]


def kernel(**inputs):
    f32 = np.float32
    x = np.ascontiguousarray(np.asarray(inputs["x"], dtype=f32))
    B = x.shape[0]
    depth = inputs["w_in"].shape[0]
    perm = wout_rows()
    for l in range(depth):
        lam_init = 0.8 - 0.6 * math.exp(-0.3 * l)
        ncM = _prog(("M", l), lambda: build_M(CFG_M, lam_init))
        w_in = np.asarray(inputs["w_in"][l])
        dn_conv = np.asarray(inputs["dn_conv"][l])
        in_maps = []
        wgs = [np.ascontiguousarray(w_in[:, wg_cols(g)]) for g in range(4)]
        for c in range(N_CORES):
            b, g = c // 4, c % 4
            dnc = np.ascontiguousarray(np.concatenate(
                [dn_conv[:, k * GW + 256 * g:k * GW + 256 * g + 256] for k in range(3)], axis=1))
            in_maps.append(dict(
                xb=x[b], attn_norm=np.asarray(inputs["attn_norm"][l]), wg=wgs[g],
                sc_conv=np.ascontiguousarray(np.asarray(inputs["sc_conv"][l])[:, 256 * g:256 * g + 256]),
                fox_bias=np.ascontiguousarray(np.asarray(inputs["fox_bias"][l])[2 * g:2 * g + 2]),
                lam_q1=np.asarray(inputs["lam_q1"][l]), lam_k1=np.asarray(inputs["lam_k1"][l]),
                lam_q2=np.asarray(inputs["lam_q2"][l]), lam_k2=np.asarray(inputs["lam_k2"][l]),
                diff_norm=np.asarray(inputs["diff_norm"][l]), dn_conv=dnc,
                dn_a_log=np.ascontiguousarray(np.asarray(inputs["dn_a_log"][l])[2 * g:2 * g + 2]),
                dn_dt_bias=np.ascontiguousarray(np.asarray(inputs["dn_dt_bias"][l])[2 * g:2 * g + 2]),
                dn_norm=np.asarray(inputs["dn_norm"][l])))
        res = run_bass_kernel_spmd(ncM, in_maps, core_ids=list(range(N_CORES)))
        ym = [np.concatenate([np.asarray(res.results[4 * b + g]["ymT"]) for g in range(4)], axis=0) for b in range(B)]
        last = (l == depth - 1)
        ncF = _prog(("F", last), lambda: build_F(CFG_F, last))
        w_out = np.ascontiguousarray(np.asarray(inputs["w_out"][l])[perm])
        shared = dict(w_out=w_out, ffn_norm=np.asarray(inputs["ffn_norm"][l]), w_gate=np.asarray(inputs["w_gate"][l]),
                      w_up=np.asarray(inputs["w_up"][l]), ffn_conv=np.asarray(inputs["ffn_conv"][l]),
                      w_down=np.asarray(inputs["w_down"][l]))
        if last:
            shared["final_norm"] = np.asarray(inputs["final_norm"])
        in_maps = []
        NT = CFG_F["NT"]
        for c in range(N_CORES):
            b, j = c // 4, c % 4
            xh = np.zeros((NT + 2, D_MODEL), f32)
            ymh = np.zeros((4096, NT + 2), ym[b].dtype)
            if j == 0:
                xh[2:] = x[b, 0:NT]
                ymh[:, 2:] = ym[b][:, 0:NT]
            else:
                xh[:] = x[b, j * NT - 2:(j + 1) * NT]
                ymh[:] = ym[b][:, j * NT - 2:(j + 1) * NT]
            m = dict(shared)
            m["xh"] = xh
            m["ymT"] = ymh
            in_maps.append(m)
        res = run_bass_kernel_spmd(ncF, in_maps, core_ids=list(range(N_CORES)))
        xn = np.empty_like(x)
        for c in range(N_CORES):
            b, j = c // 4, c % 4
            xn[b, j * NT:(j + 1) * NT] = np.asarray(res.results[c]["out"])
        x = xn
    return x
```
